# Optimizing a Trainium2 kernel written in Bass

```python
import jax, jax.numpy as jnp
from jax import lax
import numpy as np

D_MODEL = 1024
BATCH = 32
SEQ = 2048
DEPTH = 4
DEC_BATCH = 8
DEC_SEQ = 16
PAST_LEN = 1024

CHUNK = 64
N_EVEN = (DEPTH + 1) // 2
N_ODD = DEPTH // 2
SC_WIDTH = D_MODEL // 2
SC_CONV = 3
FOX_HEADS = 8
HEAD_DIM = 64
FOX_WIDTH = FOX_HEADS * HEAD_DIM
Q_BLOCK = 128
EVEN_IN = 3 * SC_WIDTH + 3 * FOX_WIDTH + FOX_HEADS
FORGET_BIAS_INIT = 3.0
D_INNER = 2 * D_MODEL
SSM_HEAD_DIM = 64
SSM_HEADS = D_INNER // SSM_HEAD_DIM
SSM_GROUPS = 4
D_STATE = 128
SSM_CONV = 4
CONV_DIM = D_INNER + 2 * SSM_GROUPS * D_STATE
ODD_IN = D_INNER + CONV_DIM + SSM_HEADS
D_FF = 2816
FFN_CONV = 3
EPS = 1e-6
RESID_SCALE = (2 * DEPTH) ** -0.5

kernel_name = 'hybrid_stream_shortconv_fox_ssd_convffn'


def rms_norm(x, gain):
    xf = x.astype(jnp.float32)
    y = xf * lax.rsqrt(jnp.mean(xf * xf, axis=-1, keepdims=True) + EPS)
    return (y * gain.astype(jnp.float32)).astype(x.dtype)


def causal_dwconv(x, state, w, bias=None):
    width = w.shape[0]
    length = x.shape[1]
    xp = jnp.concatenate([state.astype(x.dtype), x], axis=1)
    y = xp[:, 0:length] * w[0]
    for j in range(1, width):
        y = y + xp[:, j:j + length] * w[j]
    if bias is not None:
        y = y + bias
    return y, xp[:, length:]


def fox_attention(q, k, v, logf, n_past):
    b, lq, nh, hd = q.shape
    lk = k.shape[1]
    cum = jnp.cumsum(logf.astype(jnp.float32), axis=1)
    cum_k = jnp.swapaxes(cum, 1, 2)
    cum_q = cum_k[:, :, n_past:]
    qb = Q_BLOCK if lq % Q_BLOCK == 0 else lq
    nb = lq // qb
    q_blocks = jnp.moveaxis(q.reshape(b, nb, qb, nh, hd), 1, 0)
    c_blocks = jnp.moveaxis(cum_q.reshape(b, nh, nb, qb), 2, 0)
    pos_blocks = (n_past + jnp.arange(lq)).reshape(nb, qb)
    kpos = jnp.arange(lk)
    scale = HEAD_DIM ** -0.5

    def block(args):
        qblk, cblk, pblk = args
        s = jnp.einsum('bqhd,bkhd->bhqk', qblk, k).astype(jnp.float32) * scale
        s = s + cblk[..., None] - cum_k[:, :, None, :]
        s = jnp.where(kpos[None, :] <= pblk[:, None], s, -jnp.inf)
        p = jax.nn.softmax(s, axis=-1)
        return jnp.einsum('bhqk,bkhd->bqhd', p.astype(v.dtype), v)

    out = lax.map(block, (q_blocks, c_blocks, pos_blocks))
    return jnp.moveaxis(out, 0, 1).reshape(b, lq, nh, hd)


def ssd_scan(x, dt, a, bmat, cmat, h0, chunk):
    f32 = jnp.float32
    b, length, nh, hp = x.shape
    g, n = bmat.shape[2], bmat.shape[3]
    r = nh // g
    nc = length // chunk
    xc = x.astype(f32).reshape(b, nc, chunk, g, r, hp)
    dtc = dt.astype(f32).reshape(b, nc, chunk, g, r)
    bc = bmat.astype(f32).reshape(b, nc, chunk, g, n)
    cc = cmat.astype(f32).reshape(b, nc, chunk, g, n)
    cum = jnp.cumsum(dtc * a.astype(f32).reshape(g, r), axis=2)
    tri = jnp.tril(jnp.ones((chunk, chunk), dtype=bool))
    seg = cum[:, :, :, None] - cum[:, :, None, :]
    decay = jnp.exp(jnp.where(tri[:, :, None, None], seg, -jnp.inf))
    cb = jnp.einsum('bcqgn,bcsgn->bcqsg', cc, bc)
    mix = cb[..., None] * decay * dtc[:, :, None]
    y_intra = jnp.einsum('bcqsgr,bcsgrp->bcqgrp', mix, xc)
    w_end = jnp.exp(cum[:, :, -1:] - cum) * dtc
    states = jnp.einsum('bcsgn,bcsgr,bcsgrp->bcgrpn', bc, w_end, xc)
    chunk_decay = jnp.exp(cum[:, :, -1])

    def step(h, inp):
        st, dec = inp
        return dec[..., None, None] * h + st, h

    h_last, h_in = lax.scan(step, h0.astype(f32).reshape(b, g, r, hp, n),
                            (jnp.moveaxis(states, 1, 0), jnp.moveaxis(chunk_decay, 1, 0)))
    h_in = jnp.moveaxis(h_in, 0, 1)
    y_inter = jnp.einsum('bcqgn,bcgrpn,bcqgr->bcqgrp', cc, h_in, jnp.exp(cum))
    return (y_intra + y_inter).reshape(b, length, nh, hp), h_last.reshape(b, nh, hp, n)


def even_mixer(h, ck, cv, clogf, sconv_state, w_in, conv_w, q_gain, k_gain, b_f, w_out):
    b, length, _ = h.shape
    proj = h @ w_in
    cuts = [SC_WIDTH, 2 * SC_WIDTH, 3 * SC_WIDTH, 3 * SC_WIDTH + FOX_WIDTH,
            3 * SC_WIDTH + 2 * FOX_WIDTH, 3 * SC_WIDTH + 3 * FOX_WIDTH]
    gate_b, gate_c, u, q, k, v, f_logit = jnp.split(proj, cuts, axis=-1)
    conv_out, new_sconv = causal_dwconv(gate_c * u, sconv_state, conv_w)
    a_out = gate_b * conv_out
    hs = (b, length, FOX_HEADS, HEAD_DIM)
    q = rms_norm(q.reshape(hs), q_gain)
    k = rms_norm(k.reshape(hs), k_gain)
    v = v.reshape(hs)
    logf = jax.nn.log_sigmoid(f_logit.astype(jnp.float32) + b_f.astype(jnp.float32))
    n_past = ck.shape[1]
    k_all = jnp.concatenate([ck.astype(k.dtype), k], axis=1)
    v_all = jnp.concatenate([cv.astype(v.dtype), v], axis=1)
    logf_all = jnp.concatenate([clogf.astype(jnp.float32), logf], axis=1)
    attn = fox_attention(q, k_all, v_all, logf_all, n_past).reshape(b, length, FOX_WIDTH)
    out = jnp.concatenate([a_out, attn.astype(a_out.dtype)], axis=-1) @ w_out
    return (out, k.astype(ck.dtype), v.astype(cv.dtype), logf.astype(clogf.dtype),
            new_sconv.astype(sconv_state.dtype))


def odd_mixer(h, conv_state, ssm_state, w_in, conv_w, conv_b, dt_bias, a_log, d_skip, norm_w, w_out):
    f32 = jnp.float32
    b, length, _ = h.shape
    proj = h @ w_in
    z, xbc, dt_raw = jnp.split(proj, [D_INNER, D_INNER + CONV_DIM], axis=-1)
    xbc, new_conv = causal_dwconv(xbc, conv_state, conv_w, conv_b)
    xbc = jax.nn.silu(xbc)
    xs, bm, cm = jnp.split(xbc, [D_INNER, D_INNER + SSM_GROUPS * D_STATE], axis=-1)
    dt = jax.nn.softplus(dt_raw.astype(f32) + dt_bias.astype(f32))
    a = -jnp.exp(a_log.astype(f32))
    xs = xs.reshape(b, length, SSM_HEADS, SSM_HEAD_DIM)
    chunk = CHUNK if length % CHUNK == 0 else length
    y, h_last = ssd_scan(xs, dt, a,
                         bm.reshape(b, length, SSM_GROUPS, D_STATE),
                         cm.reshape(b, length, SSM_GROUPS, D_STATE), ssm_state, chunk)
    y = y + d_skip.astype(f32)[:, None] * xs.astype(f32)
    y = y.reshape(b, length, D_INNER) * jax.nn.silu(z.astype(f32))
    y = rms_norm(y.reshape(b, length, SSM_GROUPS, D_INNER // SSM_GROUPS),
                 norm_w.reshape(SSM_GROUPS, D_INNER // SSM_GROUPS)).reshape(b, length, D_INNER)
    out = y.astype(h.dtype) @ w_out
    return out, new_conv.astype(conv_state.dtype), h_last.astype(ssm_state.dtype)


def conv_ffn(h, state, w_up, conv_w, conv_b, w_down):
    a, g = jnp.split(h @ w_up, 2, axis=-1)
    a, new_state = causal_dwconv(a, state, conv_w, conv_b)
    return (jax.nn.silu(a) * g) @ w_down, new_state.astype(state.dtype)


def trunk(x, cache_k, cache_v, cache_logf, st_sconv, st_ssm_conv, st_ssm, st_ffn,
          norm_mix, norm_ffn, w_in_even, conv_a_w, q_norm, k_norm, b_forget, w_out_even,
          w_in_odd, ssm_conv_w, ssm_conv_b, dt_bias, a_log, d_skip, ssm_norm, w_out_odd,
          w_up, ffn_conv_w, ffn_conv_b, w_down):
    nk, nv, nlf, nsc, nsmc, nsm, nff = [], [], [], [], [], [], []
    for i in range(DEPTH):
        j = i // 2
        h = rms_norm(x, norm_mix[i])
        if i % 2 == 0:
            out, k, v, lf, sc = even_mixer(h, cache_k[j], cache_v[j], cache_logf[j], st_sconv[j],
                                           w_in_even[j], conv_a_w[j], q_norm[j], k_norm[j],
                                           b_forget[j], w_out_even[j])
            nk.append(k); nv.append(v); nlf.append(lf); nsc.append(sc)
        else:
            out, cs, ss = odd_mixer(h, st_ssm_conv[j], st_ssm[j], w_in_odd[j], ssm_conv_w[j],
                                    ssm_conv_b[j], dt_bias[j], a_log[j], d_skip[j], ssm_norm[j],
                                    w_out_odd[j])
            nsmc.append(cs); nsm.append(ss)
        x = x + out
        f, fs = conv_ffn(rms_norm(x, norm_ffn[i]), st_ffn[i], w_up[i], ffn_conv_w[i],
                         ffn_conv_b[i], w_down[i])
        nff.append(fs)
        x = x + f
    return (x, jnp.stack(nk), jnp.stack(nv), jnp.stack(nlf), jnp.stack(nsc),
            jnp.stack(nsmc), jnp.stack(nsm), jnp.stack(nff))


def setup_inputs(seed: int = 0) -> dict:
    key = jax.random.key(seed)
    ks = jax.random.split(key, 32)
    f32 = jnp.float32

    def nrm(k, shape, scale):
        return jax.random.normal(k, shape, f32) * scale

    dt0 = jnp.exp(jax.random.uniform(ks[20], (N_ODD, SSM_HEADS), f32,
                                     minval=np.log(1e-3), maxval=np.log(1e-1)))
    return {
        'x_prompt': nrm(ks[0], (BATCH, SEQ, D_MODEL), 1.0),
        'x_sample': nrm(ks[1], (DEC_BATCH, DEC_SEQ, D_MODEL), 1.0),
        'cache_fox_k': nrm(ks[2], (N_EVEN, DEC_BATCH, PAST_LEN, FOX_HEADS, HEAD_DIM), 1.0),
        'cache_fox_v': nrm(ks[3], (N_EVEN, DEC_BATCH, PAST_LEN, FOX_HEADS, HEAD_DIM), 1.0),
        'cache_fox_logf': jax.nn.log_sigmoid(FORGET_BIAS_INIT + nrm(ks[4], (N_EVEN, DEC_BATCH, PAST_LEN, FOX_HEADS), 1.0)),
        'state_sconv': nrm(ks[5], (N_EVEN, DEC_BATCH, SC_CONV - 1, SC_WIDTH), 1.0),
        'state_ssm_conv': nrm(ks[6], (N_ODD, DEC_BATCH, SSM_CONV - 1, CONV_DIM), 1.0),
        'state_ssm': nrm(ks[7], (N_ODD, DEC_BATCH, SSM_HEADS, SSM_HEAD_DIM, D_STATE), 0.1),
        'state_ffn_conv': nrm(ks[8], (DEPTH, DEC_BATCH, FFN_CONV - 1, D_FF), 1.0),
        'norm_mix': 1.0 + nrm(ks[9], (DEPTH, D_MODEL), 0.02),
        'norm_ffn': 1.0 + nrm(ks[10], (DEPTH, D_MODEL), 0.02),
        'w_in_even': nrm(ks[11], (N_EVEN, D_MODEL, EVEN_IN), D_MODEL ** -0.5),
        'conv_a_w': nrm(ks[12], (N_EVEN, SC_CONV, SC_WIDTH), SC_CONV ** -0.5),
        'q_norm': 1.0 + nrm(ks[13], (N_EVEN, HEAD_DIM), 0.02),
        'k_norm': 1.0 + nrm(ks[14], (N_EVEN, HEAD_DIM), 0.02),
        'b_forget': FORGET_BIAS_INIT + nrm(ks[15], (N_EVEN, FOX_HEADS), 0.1),
        'w_out_even': nrm(ks[16], (N_EVEN, SC_WIDTH + FOX_WIDTH, D_MODEL), (SC_WIDTH + FOX_WIDTH) ** -0.5 * RESID_SCALE),
        'w_in_odd': nrm(ks[17], (N_ODD, D_MODEL, ODD_IN), D_MODEL ** -0.5),
        'ssm_conv_w': nrm(ks[18], (N_ODD, SSM_CONV, CONV_DIM), SSM_CONV ** -0.5),
        'ssm_conv_b': nrm(ks[19], (N_ODD, CONV_DIM), 0.02),
        'dt_bias': dt0 + jnp.log(-jnp.expm1(-dt0)),
        'a_log': jnp.log(jax.random.uniform(ks[21], (N_ODD, SSM_HEADS), f32, minval=1.0, maxval=16.0)),
        'd_skip': 1.0 + nrm(ks[22], (N_ODD, SSM_HEADS), 0.1),
        'ssm_norm': 1.0 + nrm(ks[23], (N_ODD, D_INNER), 0.02),
        'w_out_odd': nrm(ks[24], (N_ODD, D_INNER, D_MODEL), D_INNER ** -0.5 * RESID_SCALE),
        'w_up': nrm(ks[25], (DEPTH, D_MODEL, 2 * D_FF), D_MODEL ** -0.5),
        'ffn_conv_w': nrm(ks[26], (DEPTH, FFN_CONV, D_FF), FFN_CONV ** -0.5),
        'ffn_conv_b': nrm(ks[27], (DEPTH, D_FF), 0.02),
        'w_down': nrm(ks[28], (DEPTH, D_FF, D_MODEL), D_FF ** -0.5 * RESID_SCALE),
    }


def reference(x_prompt, x_sample, cache_fox_k, cache_fox_v, cache_fox_logf, state_sconv,
              state_ssm_conv, state_ssm, state_ffn_conv, norm_mix, norm_ffn, w_in_even, conv_a_w,
              q_norm, k_norm, b_forget, w_out_even, w_in_odd, ssm_conv_w, ssm_conv_b, dt_bias,
              a_log, d_skip, ssm_norm, w_out_odd, w_up, ffn_conv_w, ffn_conv_b, w_down):
    bp = x_prompt.shape[0]
    e_k = jnp.zeros((N_EVEN, bp, 0, FOX_HEADS, HEAD_DIM), cache_fox_k.dtype)
    e_v = jnp.zeros((N_EVEN, bp, 0, FOX_HEADS, HEAD_DIM), cache_fox_v.dtype)
    e_lf = jnp.zeros((N_EVEN, bp, 0, FOX_HEADS), cache_fox_logf.dtype)
    z_sc = jnp.zeros((N_EVEN, bp) + state_sconv.shape[2:], state_sconv.dtype)
    z_smc = jnp.zeros((N_ODD, bp) + state_ssm_conv.shape[2:], state_ssm_conv.dtype)
    z_sm = jnp.zeros((N_ODD, bp) + state_ssm.shape[2:], state_ssm.dtype)
    z_ff = jnp.zeros((DEPTH, bp) + state_ffn_conv.shape[2:], state_ffn_conv.dtype)
    y_prompt, p_fox_k, p_fox_v, p_fox_logf, p_sconv, p_ssm_conv, p_ssm, p_ffn_conv = trunk(
        x_prompt, e_k, e_v, e_lf, z_sc, z_smc, z_sm, z_ff,
        norm_mix, norm_ffn, w_in_even, conv_a_w, q_norm, k_norm, b_forget, w_out_even,
        w_in_odd, ssm_conv_w, ssm_conv_b, dt_bias, a_log, d_skip, ssm_norm, w_out_odd,
        w_up, ffn_conv_w, ffn_conv_b, w_down)
    y_sample, s_fox_k, s_fox_v, s_fox_logf, s_sconv, s_ssm_conv, s_ssm, s_ffn_conv = trunk(
        x_sample, cache_fox_k, cache_fox_v, cache_fox_logf, state_sconv, state_ssm_conv,
        state_ssm, state_ffn_conv,
        norm_mix, norm_ffn, w_in_even, conv_a_w, q_norm, k_norm, b_forget, w_out_even,
        w_in_odd, ssm_conv_w, ssm_conv_b, dt_bias, a_log, d_skip, ssm_norm, w_out_odd,
        w_up, ffn_conv_w, ffn_conv_b, w_down)
    return (y_prompt, y_sample, p_fox_k, p_fox_v, p_fox_logf, p_sconv, p_ssm_conv, p_ssm, p_ffn_conv,
            s_fox_k, s_fox_v, s_fox_logf, s_sconv, s_ssm_conv, s_ssm, s_ffn_conv)
```

```python
import numpy as np
import ml_dtypes
from contextlib import ExitStack
import concourse.bass as bass
import concourse.mybir as mybir
from concourse.bass_utils import run_bass_kernel_spmd

F32 = mybir.dt.float32
BF16 = mybir.dt.bfloat16
AF = mybir.ActivationFunctionType
ALU = mybir.AluOpType
AX = mybir.AxisListType

NDMASEM = 24
SEMCAP = 16000
EPS = 1e-6


class T:
    __slots__ = ("name", "t", "lw", "rd")

    def __init__(self, name, t=None):
        self.name = name
        self.t = t
        self.lw = []
        self.rd = []

    def __getitem__(self, idx):
        return self.t[idx]


class Prog:
    def __init__(self, nc):
        self.nc = nc
        self.ops = []
        self.labels = []
        self.label = ""
        self.stack = ExitStack()

    def sb(self, name, shape, dt=F32):
        return T(name, self.stack.enter_context(self.nc.sbuf_tensor(name, list(shape), dt)))

    def ps(self, name, shape, dt=F32):
        return T(name, self.stack.enter_context(self.nc.psum_tensor(name, list(shape), dt)))

    def op(self, eng, fn, reads=(), writes=(), dma=False):
        i = len(self.ops)
        deps = set()
        for t in reads:
            deps.update(t.lw)
        for t in writes:
            deps.update(t.lw)
            deps.update(t.rd)
        for t in reads:
            t.rd.append(i)
        for t in writes:
            if dma:
                t.lw = [w for w in t.lw if self.ops[w][3]] + [i]
            else:
                t.lw = [i]
            t.rd = []
        deps.discard(i)
        self.ops.append((eng, fn, deps, dma))
        self.labels.append(self.label)
        return i

    def dma(self, q, out, in_, reads=(), writes=(), **kw):
        return self.op(q, lambda e: e.dma_start(out=out, in_=in_, **kw), reads, writes, dma=True)

    def emit(self):
        nc = self.nc
        ops = self.ops
        n = len(ops)
        engs = ["pe", "act", "dve", "pool", "sp"]
        signal = [False] * n
        for i, (eng, fn, deps, dma) in enumerate(ops):
            for d in deps:
                deng, _, _, ddma = ops[d]
                if ddma:
                    continue
                if deng != eng or dma or eng != "pe":
                    signal[d] = True
        cnt = [None] * n
        run = {e: 0 for e in engs}
        dk = {e: 0 for e in engs}
        dslot = [None] * n
        for i, (eng, fn, deps, dma) in enumerate(ops):
            if dma:
                k = dk[eng]
                dk[eng] += 1
                dslot[i] = (eng, k % NDMASEM, 16 * (k // NDMASEM + 1))
            elif signal[i]:
                c = run[eng]
                run[eng] += 1
                cnt[i] = (c // SEMCAP, c % SEMCAP + 1)
        st = self.stack
        esem = {}
        for e in ["pe", "act", "dve", "pool"]:
            esem[e] = [st.enter_context(nc.semaphore("s_%s_%d" % (e, j))) for j in range(run[e] // SEMCAP + 1)]
        dsem = {}
        for q in engs:
            if dk[q] > 0:
                dsem[q] = [st.enter_context(nc.semaphore("d_%s_%d" % (q, j))) for j in range(min(NDMASEM, dk[q]))]
        per = {e: [i for i in range(n) if ops[i][0] == e] for e in engs}
        final_waits = []
        for q in dsem:
            for j, sm in enumerate(dsem[q]):
                uses = (dk[q] - 1 - j) // NDMASEM + 1
                final_waits.append((sm, 16 * uses))

        def run_engine(e, name):
            waited = {}

            def need(sm, val):
                key = sm.name
                if waited.get(key, 0) >= val:
                    return
                e.wait_ge(sm, val)
                waited[key] = val

            for i in per[name]:
                eng, fn, deps, dma = ops[i]
                for d in sorted(deps):
                    deng, _, _, ddma = ops[d]
                    if ddma:
                        q, slot, val = dslot[d]
                        need(dsem[q][slot], val)
                    elif deng != eng or dma or eng != "pe":
                        si, val = cnt[d]
                        need(esem[deng][si], val)
                if dma:
                    q, slot, val = dslot[i]
                    if val > 16:
                        need(dsem[q][slot], val - 16)
                    fn(e).then_inc(dsem[q][slot], 16)
                else:
                    ins = fn(e)
                    if signal[i]:
                        ins.then_inc(esem[eng][cnt[i][0]], 1)
            if name == "sp":
                for sm, val in final_waits:
                    need(sm, val)

        with nc.Block() as block:
            @block.tensor
            def _(e):
                run_engine(e, "pe")

            @block.scalar
            def _(e):
                run_engine(e, "act")

            @block.vector
            def _(e):
                run_engine(e, "dve")

            @block.gpsimd
            def _(e):
                run_engine(e, "pool")

            @block.sync
            def _(e):
                run_engine(e, "sp")
        self.stack.close()
        return dict(n_ops=n, signals=sum(signal), per={k: len(v) for k, v in per.items()})


class Rot:
    def __init__(self, bufs):
        self.bufs = bufs
        self.i = 0

    def get(self):
        b = self.bufs[self.i % len(self.bufs)]
        self.i += 1
        return b


FULL_PLAN = [("even", 0), ("ffn", 0), ("odd", 0), ("ffn", 1), ("even", 1), ("ffn", 2), ("odd", 1), ("ffn", 3)]

WSHAPES = {
    "w_es": [2, 4, 128, 8 * 3 * 128], "w_eqk": [2, 2, 128, 8 * 512], "w_evf": [2, 1, 128, 8 * 520],
    "w_eo": [2, 8, 128, 8 * 128], "w_ox": [2, 24, 128, 8 * 128], "w_oz": [2, 4, 128, 8 * 512],
    "w_odt": [2, 1, 128, 8 * 32], "w_oo": [2, 8, 128, 16 * 128], "w_fu": [4, 22, 128, 8 * 256],
    "w_fd": [4, 8, 128, 22 * 128],
}
PSHAPES = {
    "g_mix": [4, 128, 8], "g_ffn": [4, 128, 8], "cw_a": [2, 128, 12], "qg": [2, 128, 64], "kg": [2, 128, 64],
    "bfg": [2, 128, 8], "cw_x": [2, 128, 96], "cb_x": [2, 128, 24], "dtb": [2, 128, 32], "alog": [2, 128, 32],
    "dsk": [2, 128, 32], "nw": [2, 128, 2048], "cw_f": [4, 128, 66], "cb_f": [4, 128, 22],
}
CSHAPES = {"c_ident": [128, 128], "c_tri": [128, 128], "c_ones": [128, 128], "c_maskneg": [128, 128],
           "c_u": [128, 128], "c_sel": [8, 8 * 128]}
PAST = 1024


def build(NP, L, plan=FULL_PLAN, with_sample=True):
    nc = bass.Bass("TRN2", target_bir_lowering=False)
    P = Prog(nc)

    def din(name, shape, dt=F32):
        return nc.dram_tensor(name, list(shape), dt, kind="ExternalInput").ap()

    def dout(name, shape):
        return nc.dram_tensor(name, list(shape), F32, kind="ExternalOutput").ap()

    D = {}
    D["xp"] = din("xp", [NP, 128, 8 * L])
    D["xs"] = din("xs", [1, 128, 8 * 16])
    D["ckT"] = din("ckT", [2, 128, 4 * PAST])
    D["cv"] = din("cv", [2, PAST, 512])
    D["clf"] = din("clf", [2, PAST, 8])
    D["st_sc"] = din("st_sc", [2, 128, 8])
    D["st_xc"] = din("st_xc", [2, 128, 72])
    D["st_ssm"] = din("st_ssm", [2, 128, 2048])
    D["st_fc"] = din("st_fc", [4, 128, 44])
    for k, s in list(WSHAPES.items()) + list(PSHAPES.items()) + list(CSHAPES.items()):
        D[k] = din(k, s)
    WB = {k: nc.dram_tensor(k + "_b", list(s), BF16, kind="Internal").ap() for k, s in WSHAPES.items()}
    WTOK = {}
    O = {}
    for pre, nb, ll in (("p", NP, L), ("s", 1, 16)):
        O[pre + "_y"] = dout(pre + "_y", [nb, 128, 8 * ll])
        O[pre + "_k"] = dout(pre + "_k", [2, nb, ll, 512])
        O[pre + "_v"] = dout(pre + "_v", [2, nb, ll, 512])
        O[pre + "_lf"] = dout(pre + "_lf", [2, nb, ll, 8])
        O[pre + "_sc"] = dout(pre + "_sc", [2, nb, 128, 8])
        O[pre + "_xc"] = dout(pre + "_xc", [2, nb, 128, 72])
        O[pre + "_ssm"] = dout(pre + "_ssm", [2, nb, 128, 2048])
        O[pre + "_fc"] = dout(pre + "_fc", [4, nb, 128, 44])

    def const_f(name, shape, src):
        t = P.sb(name, shape, F32)
        P.dma("sp", t[:], src, writes=[t])
        return t

    def const_b(name, shape, src):
        t = P.sb(name, shape, BF16)
        P.dma("pool", t[:], src, writes=[t])
        return t

    IDF = const_f("idf", [128, 128], D["c_ident"][:, :])
    TRI = const_f("tri", [128, 128], D["c_tri"][:, :])
    ONF = const_f("onf", [128, 128], D["c_ones"][:, :])
    IDB = const_b("idb", [128, 128], D["c_ident"][:, :])
    ONB = const_b("onb", [128, 128], D["c_ones"][:, :])
    MNB = const_b("mnb", [128, 128], D["c_maskneg"][:, :])
    UB = const_b("ub", [128, 128], D["c_u"][:, :])
    SELB = const_b("selb", [8, 8 * 128], D["c_sel"][:, :])

    TT = 512
    XT = [P.sb("xt%d" % i, [128, 8, TT]) for i in range(max(1, L // TT))]
    XS = P.sb("xs_t", [128, 8, 16])
    HTS2 = [P.sb("ht%d" % i, [128, 8, TT], BF16) for i in range(2)]
    sqr = Rot([P.sb("sq%d" % i, [128, TT], BF16) for i in range(2)])
    RS = P.sb("rs", [128, TT])
    GMS = [P.sb("gm%d" % i, [128, 8]) for i in range(2)]
    PSF = [P.ps("psf%d" % i, [128, 512], F32) for i in range(6)]
    psf = Rot(PSF[0:5])
    psf6 = Rot(PSF)
    psr4 = Rot(PSF[0:4])
    psb = Rot([P.ps("psb%d" % i, [128, 1024], BF16) for i in range(2)])
    wbuf = Rot([P.sb("wbuf%d" % i, [128, 4160], BF16) for i in range(3)])
    SCR = [P.sb("scr%d" % i, [128, 516]) for i in range(8)]
    ARENA = P.sb("arena", [128, 27136], BF16)
    PH = {"even": [], "odd": [], "ffn": []}

    def aview(kind, name, off, shape):
        size = 1
        for d in shape[1:]:
            size *= d
        ap = ARENA[0:shape[0], off:off + size]
        if len(shape) == 3:
            ap = ap.rearrange("p (a b) -> p a b", a=shape[1])
        t = T(name, ap)
        PH[kind].append(t)
        return t

    def fence(kind):
        old = set()
        for k2, toks in PH.items():
            if k2 != kind:
                for t in toks:
                    old.update(t.lw)
                    old.update(t.rd)
        for t in PH[kind]:
            t.rd = list(set(t.rd) | old)

    def wload(name, i, s, cols):
        t = wbuf.get()
        P.dma("sp", t[:, 0:cols], WB[name][i, s], reads=[WTOK[(name, i, s)]], writes=[t])
        return t

    def pload(name, i, cols, tile):
        P.dma("sp", tile[:, 0:cols], D[name][i], writes=[tile])

    def L_(x):
        return list(x) if isinstance(x, (list, tuple)) else [x]

    def mm(pt, out, lhsT, rhs, start, stop, reads):
        P.op("pe", lambda e: e.matmul(out, lhsT, rhs, start=start, stop=stop), reads, [pt])

    def trp(pt, out, in_, ident, reads):
        P.op("pe", lambda e: e.transpose(out, in_, ident), reads, [pt])

    def act(ot, out, in_, func, reads, bias=None, scale=None):
        kw = {}
        if bias is not None:
            kw["bias"] = bias
        if scale is not None:
            kw["scale"] = scale
        P.op("act", lambda e: e.activation(out=out, in_=in_, func=func, **kw), reads, L_(ot))

    def tt(eng, ot, out, in0, in1, op, reads):
        P.op(eng, lambda e: e.tensor_tensor(out=out, in0=in0, in1=in1, op=op), reads, [ot])

    def ts(eng, ot, out, in0, s1, s2, op0, op1, reads):
        if op1 is None:
            P.op(eng, lambda e: e.tensor_scalar(out=out, in0=in0, scalar1=s1, scalar2=None, op0=op0), reads, [ot])
        else:
            P.op(eng, lambda e: e.tensor_scalar(out=out, in0=in0, scalar1=s1, scalar2=s2, op0=op0, op1=op1), reads, [ot])

    def stt(eng, ot, out, in0, scalar, in1, op0, op1, reads):
        P.op(eng, lambda e: e.scalar_tensor_tensor(out=out, in0=in0, scalar=scalar, in1=in1, op0=op0, op1=op1), reads, [ot])

    def cp(eng, ot, out, in_, reads):
        if eng == "act":
            P.op("act", lambda e: e.copy(out=out, in_=in_), reads, L_(ot))
        else:
            P.op(eng, lambda e: e.tensor_copy(out=out, in_=in_), reads, L_(ot))

    def red(eng, ot, out, in_, reads):
        P.op(eng, lambda e: e.tensor_reduce(out=out, in_=in_, axis=AX.X, op=ALU.add), reads, [ot])

    def rstd_inplace(eng, t, ap, inv_n, reads):
        ts(eng, t, ap, ap, inv_n, EPS, ALU.mult, ALU.add, reads)
        act(t, ap, ap, AF.Ln, [t])
        act(t, ap, ap, AF.Exp, [t], scale=-0.5)

    def xtile(seq, ti):
        return XS if seq["sample"] else XT[ti]

    NPS = PSF[5]

    def norm_chunks(seq, ti, n, cs):
        xt = xtile(seq, ti)
        for c in cs:
            sq = sqr.get()
            act(sq, sq[:, 0:n], xt[:, c, 0:n], AF.Square, [xt])
            mm(NPS, NPS[:, 0:n], ONB[:, :], sq[:, 0:n], c == 0, c == 7, [ONB, sq])

    def norm_final(seq, ti, n, gname, layer, HT, GM):
        xt = xtile(seq, ti)
        pload(gname, layer, 8, GM)
        ts("dve", RS, RS[:, 0:n], NPS[:, 0:n], 1.0 / 1024, EPS, ALU.mult, ALU.add, [NPS])
        act(RS, RS[:, 0:n], RS[:, 0:n], AF.Ln, [RS])
        act(RS, RS[:, 0:n], RS[:, 0:n], AF.Exp, [RS], scale=-0.5)
        for c in range(8):
            stt("dve", HT, HT[:, c, 0:n], xt[:, c, 0:n], GM[:, c:c + 1], RS[:, 0:n], ALU.mult, ALU.mult, [xt, GM, RS])

    def rmsnorm(seq, ti, n, gname, layer, HT, GM):
        norm_chunks(seq, ti, n, range(8))
        norm_final(seq, ti, n, gname, layer, HT, GM)

    def resid_add(seq, ti, n, dc, pt):
        xt = xtile(seq, ti)
        tt("dve", xt, xt[:, dc, 0:n], xt[:, dc, 0:n], pt[:, 0:n], ALU.add, [xt, pt])

    MT = aview("ffn", "mt", 0, [128, 22, TT])
    HALO_F = P.sb("halo_f", [128, 22, 2])
    CWF = P.sb("cwf", [128, 66])
    CBF = P.sb("cbf", [128, 22])
    rawf = Rot(SCR[0:3])
    cvf = Rot(SCR[3:6])

    def ffn_tile(seq, ti, n, i, first, HT, prefetch):
        if first:
            fence("ffn")
            pload("cw_f", i, 66, CWF)
            pload("cb_f", i, 22, CBF)
            if seq["sample"]:
                P.dma("sp", HALO_F[:].rearrange("p a b -> p (a b)"), D["st_fc"][i], writes=[HALO_F])
            else:
                P.op("pool", lambda e: e.memset(HALO_F[:], 0.0), [], [HALO_F])
        P.label = "ffn.up"
        pend = None
        for j in range(22):
            w = wload("w_fu", i, j, 2048)
            wv = w[:, 0:2048].rearrange("p (c k f) -> p c k f", c=8, k=2)
            pa = psf6.get()
            pg = psf6.get()
            for c in range(8):
                mm(pa, pa[:, 0:n], wv[:, c, 0, :], HT[:, c, 0:n], c == 0, c == 7, [w, HT])
            for c in range(8):
                mm(pg, pg[:, 0:n], wv[:, c, 1, :], HT[:, c, 0:n], c == 0, c == 7, [w, HT])
            raw = rawf.get()
            cp("pool", raw, raw[:, 0:2], HALO_F[:, j, :], [HALO_F])
            cp("act", raw, raw[:, 2:2 + n], pa[:, 0:n], [pa])
            cp("pool", HALO_F, HALO_F[:, j, :], raw[:, n:n + 2], [raw])
            cv = cvf.get()
            ts("pool", cv, cv[:, 0:n], raw[:, 0:n], CWF[:, 3 * j:3 * j + 1], CBF[:, j:j + 1], ALU.mult, ALU.add, [raw, CWF, CBF])
            stt("dve", cv, cv[:, 0:n], raw[:, 1:1 + n], CWF[:, 3 * j + 1:3 * j + 2], cv[:, 0:n], ALU.mult, ALU.add, [raw, CWF, cv])
            stt("dve", cv, cv[:, 0:n], raw[:, 2:2 + n], CWF[:, 3 * j + 2:3 * j + 3], cv[:, 0:n], ALU.mult, ALU.add, [raw, CWF, cv])

            def tail(cv=cv, pg=pg, j=j):
                act(cv, cv[:, 0:n], cv[:, 0:n], AF.Silu, [cv])
                tt("dve", MT, MT[:, j, 0:n], cv[:, 0:n], pg[:, 0:n], ALU.mult, [cv, pg])

            if pend is not None:
                pend()
            pend = tail
        pend()
        P.label = "ffn.down"
        for dc in range(8):
            prefetch(dc)
            w = wload("w_fd", i, dc, 22 * 128)
            wv = w[:, 0:2816].rearrange("p (j f) -> p j f", j=22)
            pt = psf.get()
            for j in range(22):
                mm(pt, pt[:, 0:n], wv[:, j, :], MT[:, j, 0:n], j == 0, j == 21, [w, MT])
            resid_add(seq, ti, n, dc, pt)

    def ffn_finish(seq, i):
        P.dma("sp", seq["o_fc"][i], HALO_F[:].rearrange("p a b -> p (a b)"), reads=[HALO_F])

    NKB = 16
    KT = aview("even", "kt", 0, [128, 4, 2048])
    VV = aview("even", "vv", 8192, [128, 16, 512])
    QT = aview("even", "qt", 16384, [128, 4, TT])
    AO = aview("even", "ao", 18432, [128, 8, TT])
    CQT = aview("even", "cqt", 22528, [8, 2048])
    qn_r = Rot([aview("even", "qn%d" % i, 24576 + 512 * i, [128, 512]) for i in range(2)])
    pt_r = Rot([aview("even", "ptb%d" % i, 25600 + 512 * i, [128, 512]) for i in range(3)])
    CUM = P.sb("cum", [128, NKB, 8])
    NCUM = P.sb("ncum", [128, NKB, 8])
    CARRY = P.sb("carry", [128, 8])
    HALO_S = P.sb("halo_s", [128, 4, 2])
    CWA = P.sb("cwa", [128, 12])
    QG = P.sb("qg_t", [128, 64])
    KG = P.sb("kg_t", [128, 64])
    BFG = P.sb("bfg_t", [128, 8])
    LFP = P.sb("lfp", [128, 8, 8])
    us_r = Rot(SCR[0:2])
    cur_r = Rot(SCR[2:4])
    cva_r = Rot(SCR[4:6])
    sq_r = Rot(SCR[6:7])
    t1_r = Rot(SCR[7:8])
    kn_r = Rot(SCR[0:2])
    vo_r = Rot(SCR[2:4])
    rl_r = Rot(SCR[4:6])
    ss_r = Rot([P.sb("ssq%d" % i, [128, 8]) for i in range(2)])
    lf_r = Rot([P.sb("lf%d" % i, [128, 8]) for i in range(2)])
    sm_r = Rot([P.sb("sm%d" % i, [128, 8]) for i in range(2)])
    cq_r = Rot([P.sb("cq%d" % i, [128, 8], BF16) for i in range(2)])

    def cum_block(lf_t, lf_ap, bs, kb):
        pA = psf.get()
        mm(pA, pA[0:bs, 0:8], TRI[0:bs, 0:bs], lf_ap, True, True, [TRI, lf_t])
        pB = psf.get()
        mm(pB, pB[:, 0:8], ONF[0:bs, :], lf_ap, True, True, [ONF, lf_t])
        tt("dve", CUM, CUM[0:bs, kb, :], pA[0:bs, 0:8], CARRY[0:bs, :], ALU.add, [pA, CARRY])
        ts("dve", NCUM, NCUM[0:bs, kb, :], CUM[0:bs, kb, :], -1.0, None, ALU.mult, None, [CUM])
        tt("dve", CARRY, CARRY[:, :], CARRY[:, :], pB[:, 0:8], ALU.add, [CARRY, pB])

    def even_tile(seq, ti, n, j, first, HT, prefetch):
        sample = seq["sample"]
        past = PAST if sample else 0
        t0 = ti * TT
        if first:
            fence("even")
            pload("cw_a", j, 12, CWA)
            pload("qg", j, 64, QG)
            pload("kg", j, 64, KG)
            pload("bfg", j, 8, BFG)
            P.op("dve", lambda e: e.memset(CARRY[:], 0.0), [], [CARRY])
            if sample:
                P.dma("sp", HALO_S[:].rearrange("p a b -> p (a b)"), D["st_sc"][j], writes=[HALO_S])
                P.dma("pool", KT[:, :, 0:PAST], D["ckT"][j].rearrange("p (a t) -> p a t", a=4), writes=[KT])
                P.dma("pool", VV[:, 0:PAST // 128, :], D["cv"][j].rearrange("(kb p) f -> p kb f", p=128), writes=[VV])
                P.dma("sp", LFP[:], D["clf"][j].rearrange("(kb p) h -> p kb h", p=128), writes=[LFP])
                for kb in range(PAST // 128):
                    cum_block(LFP, LFP[:, kb, :], 128, kb)
            else:
                P.op("pool", lambda e: e.memset(HALO_S[:], 0.0), [], [HALO_S])
        P.label = "even.A"
        for cc in range(4):
            w = wload("w_es", j, cc, 8 * 3 * 128)
            wv = w[:, 0:3072].rearrange("p (c k f) -> p c k f", c=8, k=3)
            pb_, pc_, pu_ = psf.get(), psf.get(), psf.get()
            for k, pt in enumerate((pb_, pc_, pu_)):
                for c in range(8):
                    mm(pt, pt[:, 0:n], wv[:, c, k, :], HT[:, c, 0:n], c == 0, c == 7, [w, HT])
            us = us_r.get()
            cp("act", us, us[:, 0:n], pu_[:, 0:n], [pu_])
            cur = cur_r.get()
            cp("pool", cur, cur[:, 0:2], HALO_S[:, cc, :], [HALO_S])
            tt("dve", cur, cur[:, 2:2 + n], pc_[:, 0:n], us[:, 0:n], ALU.mult, [pc_, us])
            cp("pool", HALO_S, HALO_S[:, cc, :], cur[:, n:n + 2], [cur])
            cv = cva_r.get()
            ts("pool", cv, cv[:, 0:n], cur[:, 0:n], CWA[:, 3 * cc:3 * cc + 1], None, ALU.mult, None, [cur, CWA])
            stt("dve", cv, cv[:, 0:n], cur[:, 1:1 + n], CWA[:, 3 * cc + 1:3 * cc + 2], cv[:, 0:n], ALU.mult, ALU.add, [cur, CWA, cv])
            stt("dve", cv, cv[:, 0:n], cur[:, 2:2 + n], CWA[:, 3 * cc + 2:3 * cc + 3], cv[:, 0:n], ALU.mult, ALU.add, [cur, CWA, cv])
            tt("dve", AO, AO[:, cc, 0:n], pb_[:, 0:n], cv[:, 0:n], ALU.mult, [pb_, cv])
        P.label = "even.B"
        bs = min(128, n)
        nblk = n // bs
        def q_tail(qn, c0):
            pq = psb.get()
            for pr in range(4):
                trp(pq, pq[:, pr * bs:(pr + 1) * bs], qn[0:bs, pr * 128:(pr + 1) * 128], IDB[0:bs, 0:bs], [qn, IDB])
            cp("act", QT, QT[:, :, c0:c0 + bs], pq[:, 0:4 * bs].rearrange("p (a t) -> p a t", a=4), [pq])

        def k_tail(kn, c0):
            tok0 = t0 + c0
            pk = psf.get()
            for pr in range(4):
                trp(pk, pk[:, pr * bs:(pr + 1) * bs], kn[0:bs, pr * 128:(pr + 1) * 128], IDF[0:bs, 0:bs], [kn, IDF])
            cp("act", KT, KT[:, :, past + tok0:past + tok0 + bs], pk[:, 0:4 * bs].rearrange("p (a t) -> p a t", a=4), [pk])

        wts3 = {"q": wload("w_eqk", j, 0, 8 * 512), "k": wload("w_eqk", j, 1, 8 * 512), "v": wload("w_evf", j, 0, 8 * 520)}
        for which in ("q", "k"):
            wt = wts3[which]
            wview = wt[:, 0:4096].rearrange("p (c f) -> p c f", c=8)
            pend = None
            for b in range(nblk):
                c0 = b * bs
                tok0 = t0 + c0
                pt = psf.get()
                for c in range(8):
                    mm(pt, pt[0:bs, :], HT[:, c, c0:c0 + bs], wview[:, c, :], c == 0, c == 7, [HT, wt])
                sq = sq_r.get()
                act(sq, sq[0:bs, 0:512], pt[0:bs, :], AF.Square, [pt])
                ss = ss_r.get()
                red("dve", ss, ss[0:bs, :], sq[0:bs, 0:512].rearrange("p (h d) -> p h d", h=8), [sq])
                rstd_inplace("dve", ss, ss[0:bs, :], 1.0 / 64, [ss])
                t1 = t1_r.get()
                tt("dve", t1, t1[0:bs, 0:512].rearrange("p (h d) -> p h d", h=8), pt[0:bs, :].rearrange("p (h d) -> p h d", h=8),
                   ss[0:bs, :].unsqueeze(2).to_broadcast([bs, 8, 64]), ALU.mult, [pt, ss])
                if which == "q":
                    qn = qn_r.get()
                    tt("pool", qn, qn[0:bs, :].rearrange("p (h d) -> p h d", h=8), t1[0:bs, 0:512].rearrange("p (h d) -> p h d", h=8),
                       QG[0:bs, None, :].to_broadcast([bs, 8, 64]), ALU.mult, [t1, QG])
                    cur_ = (q_tail, qn, c0)
                else:
                    kn = kn_r.get()
                    tt("pool", kn, kn[0:bs, 0:512].rearrange("p (h d) -> p h d", h=8), t1[0:bs, 0:512].rearrange("p (h d) -> p h d", h=8),
                       KG[0:bs, None, :].to_broadcast([bs, 8, 64]), ALU.mult, [t1, KG])
                    P.dma("sp", seq["o_k"][j][tok0:tok0 + bs, :], kn[0:bs, 0:512], reads=[kn])
                    cur_ = (k_tail, kn, c0)
                if pend is not None:
                    pend[0](pend[1], pend[2])
                pend = cur_
            pend[0](pend[1], pend[2])
        wvf = wts3["v"]
        wvv = wvf[:, 0:4160].rearrange("p (c f) -> p c f", c=8)

        def v_tail(lf, c0):
            kb = (past + t0 + c0) // 128
            tok0 = t0 + c0
            cum_block(lf, lf[0:bs, :], bs, kb)
            cq = cq_r.get()
            ts("dve", cq, cq[0:bs, :], CUM[0:bs, kb, :], 8.0, None, ALU.mult, None, [CUM])
            pc8 = psb.get()
            trp(pc8, pc8[0:8, 0:bs], cq[0:bs, :], IDB[0:bs, 0:bs], [cq, IDB])
            cp("act", CQT, CQT[0:8, tok0:tok0 + bs], pc8[0:8, 0:bs], [pc8])

        pend = None
        for b in range(nblk):
            c0 = b * bs
            kb = (past + t0 + c0) // 128
            tok0 = t0 + c0
            pv = psf.get()
            for c in range(8):
                mm(pv, pv[0:bs, :], HT[:, c, c0:c0 + bs], wvv[:, c, 0:512], c == 0, c == 7, [HT, wvf])
            pf = psf.get()
            for c in range(8):
                mm(pf, pf[0:bs, 0:8], HT[:, c, c0:c0 + bs], wvv[:, c, 512:520], c == 0, c == 7, [HT, wvf])
            vo = vo_r.get()
            cp("act", vo, vo[0:bs, 0:512], pv[0:bs, :], [pv])
            P.dma("sp", seq["o_v"][j][tok0:tok0 + bs, :], vo[0:bs, 0:512], reads=[vo])
            cp("pool", VV, VV[0:bs, kb, :], vo[0:bs, 0:512], [vo])
            s1 = sm_r.get()
            tt("dve", s1, s1[0:bs, :], pf[0:bs, 0:8], BFG[0:bs, :], ALU.add, [pf, BFG])
            act(s1, s1[0:bs, :], s1[0:bs, :], AF.Exp, [s1], scale=-1.0)
            act(s1, s1[0:bs, :], s1[0:bs, :], AF.Ln, [s1], bias=1.0, scale=1.0)
            lf = lf_r.get()
            ts("dve", lf, lf[0:bs, :], s1[0:bs, :], -1.0, None, ALU.mult, None, [s1])
            P.dma("sp", seq["o_lf"][j][tok0:tok0 + bs, :], lf[0:bs, :], reads=[lf])
            if pend is not None:
                v_tail(*pend)
            pend = (lf, c0)
        v_tail(*pend)
        P.label = "even.C"
        kblocks = []
        for kb in range(past // 128):
            kblocks.append((kb, kb * 128, 128, 0, False))
        for b in range((t0 + n + bs - 1) // bs):
            k0 = b * bs
            if k0 >= t0 + n:
                break
            qoff = max(0, k0 - t0)
            kblocks.append(((past + k0) // 128, past + k0, bs, qoff, k0 >= t0))
        po, pl = PSF[4], PSF[5]
        pend_pv = [None]

        def flush_pv():
            if pend_pv[0] is not None:
                pend_pv[0]()
                pend_pv[0] = None

        for h in range(8):
            pr, hp = h // 2, h % 2
            for idx, (kb, kc0, kbs, qoff, diag) in enumerate(kblocks):
                ps_ = psr4.get()
                mm(ps_, ps_[0:kbs, qoff:n], KT[hp * 64:hp * 64 + 64, pr, kc0:kc0 + kbs], QT[hp * 64:hp * 64 + 64, pr, qoff:n], True, False, [KT, QT])
                if diag:
                    mm(ps_, ps_[0:kbs, qoff:qoff + kbs], IDB[0:kbs, 0:kbs], MNB[0:kbs, 0:kbs], False, False, [IDB, MNB])
                mm(ps_, ps_[0:kbs, qoff:n], SELB[0:8, h * 128:h * 128 + kbs], CQT[0:8, t0 + qoff:t0 + n], False, True, [SELB, CQT])
                ptb = pt_r.get()
                act(ptb, ptb[0:kbs, qoff:n], ps_[0:kbs, qoff:n], AF.Exp, [ps_, NCUM], bias=NCUM[0:kbs, kb, h:h + 1], scale=0.125)
                flush_pv()
                last = idx == len(kblocks) - 1

                def pv(ptb=ptb, kb=kb, kbs=kbs, qoff=qoff, idx=idx, last=last, pr=pr, hp=hp):
                    mm(po, po[:, qoff:n], VV[0:kbs, kb, pr * 128:(pr + 1) * 128], ptb[0:kbs, qoff:n], idx == 0, last, [VV, ptb])
                    mm(pl, pl[:, qoff:n], ONB[0:kbs, :], ptb[0:kbs, qoff:n], idx == 0, last, [ONB, ptb])
                    if last:
                        rl = rl_r.get()
                        lo, hi = hp * 64, hp * 64 + 64
                        P.op("dve", (lambda e, rl=rl, lo=lo, hi=hi: e.reciprocal(out=rl[lo:hi, 0:n], in_=pl[lo:hi, 0:n])), [pl], [rl])
                        tt("dve", AO, AO[lo:hi, 4 + pr, 0:n], po[lo:hi, 0:n], rl[lo:hi, 0:n], ALU.mult, [po, rl])

                pend_pv[0] = pv
        flush_pv()
        P.label = "even.D"
        for dc in range(8):
            prefetch(dc)
            w = wload("w_eo", j, dc, 8 * 128)
            wv = w[:, 0:1024].rearrange("p (c f) -> p c f", c=8)
            pt = psf.get()
            for cc in range(8):
                mm(pt, pt[:, 0:n], wv[:, cc, :], AO[:, cc, 0:n], cc == 0, cc == 7, [w, AO])
            resid_add(seq, ti, n, dc, pt)

    def even_finish(seq, j):
        P.dma("sp", seq["o_sc"][j], HALO_S[:].rearrange("p a b -> p (a b)"), reads=[HALO_S])

    XBC = aview("odd", "xbc", 0, [128, 24, TT])
    XTK = {}
    for grp in range(12):
        for bb in range(4):
            tk = T("xbc_%d_%d" % (grp, bb))
            PH["odd"].append(tk)
            XTK[(grp, bb)] = tk

    def xgrp(fc):
        return fc // 4 if fc < 16 else (4 + fc - 16)

    HTSB = aview("odd", "htsb", 12288, [128, 4, 512])
    xtok_r = Rot([aview("odd", "xtok%d" % i, 14336 + 512 * i, [128, 512]) for i in range(2)])
    btok_r = Rot([aview("odd", "btok%d" % i, 15360 + 128 * i, [128, 128]) for i in range(2)])
    xdt_r = Rot([aview("odd", "xdt%d" % i, 15616 + 512 * i, [128, 512]) for i in range(2)])
    xw_r = Rot([aview("odd", "xw%d" % i, 16640 + 512 * i, [128, 512]) for i in range(2)])
    cbm_r = Rot([aview("odd", "cbm%d" % i, 17664 + 128 * i, [128, 128]) for i in range(2)])
    yb_r = Rot([aview("odd", "yb%d" % i, 17920 + 1024 * i, [128, 8, 128]) for i in range(2)])
    lt_r = Rot([aview("odd", "lt%d" % i, 19968 + 1024 * i, [128, 8, 128]) for i in range(2)])
    mix_r = Rot([aview("odd", "mix%d" % i, 22016 + 1024 * i, [128, 8, 128]) for i in range(2)])
    ynb_r = Rot([aview("odd", "ynb%d" % i, 24064 + 512 * i, [128, 512]) for i in range(2)])
    xdsk_r = Rot([aview("odd", "xdsk%d" % i, 25088 + 512 * i, [128, 512]) for i in range(2)])
    HALO_X = P.sb("halo_x", [128, 24, 3])
    CWX = P.sb("cwx", [128, 96])
    CBX = P.sb("cbx", [128, 24])
    DTB = P.sb("dtb_t", [128, 32])
    ANEG = P.sb("aneg", [128, 32])
    DSK = P.sb("dsk_t", [128, 32])
    nw_r = Rot([P.sb("nw_t%d" % i, [128, 512]) for i in range(2)])
    HTS = P.sb("hts", [128, 4, 512])
    DTs = P.sb("dts", [128, 4, 32])
    DTA = P.sb("dta", [128, 4, 32])
    EXPC = P.sb("expc", [128, 4, 32])
    CDEC = P.sb("cdec", [128, 4, 32])
    WEND = P.sb("wend", [128, 4, 32])
    rawx = Rot(SCR[0:2])
    cvx = Rot([SCR[2], SCR[3], SCR[6]])
    y1_r = Rot(SCR[4:6])
    y2_r = Rot(SCR[6:7])
    sz_r = Rot(SCR[2:4])
    sqs_r = Rot(SCR[7:8])
    sg_r = Rot(SCR[6:7])
    tmph_r = Rot(SCR[0:2])
    s32_r = Rot([P.sb("s32_%d" % i, [128, 32]) for i in range(4)])
    s1_r = Rot([P.sb("s1_%d" % i, [128, 1]) for i in range(2)])

    def odd_tile(seq, ti, n, j, first, HT, prefetch):
        sample = seq["sample"]
        bs = min(128, n)
        nblk = n // bs
        if first:
            fence("odd")
            pload("cw_x", j, 96, CWX)
            pload("cb_x", j, 24, CBX)
            pload("dtb", j, 32, DTB)
            pload("alog", j, 32, ANEG)
            pload("dsk", j, 32, DSK)
            act(ANEG, ANEG[:, :], ANEG[:, :], AF.Exp, [ANEG])
            ts("dve", ANEG, ANEG[:, :], ANEG[:, :], -1.0, None, ALU.mult, None, [ANEG])
            if sample:
                P.dma("sp", HALO_X[:].rearrange("p a b -> p (a b)"), D["st_xc"][j], writes=[HALO_X])
                P.dma("sp", HTS[:].rearrange("p a b -> p (a b)"), D["st_ssm"][j], writes=[HTS])
            else:
                P.op("pool", lambda e: e.memset(HALO_X[:], 0.0), [], [HALO_X])
                P.op("pool", lambda e: e.memset(HTS[:], 0.0), [], [HTS])
            cp("act", HTSB, HTSB[:], HTS[:], [HTS])
        P.label = "odd.dt"
        wdt = wload("w_odt", j, 0, 8 * 32)
        wdv = wdt[:, 0:256].rearrange("p (c f) -> p c f", c=8)
        for b in range(nblk):
            c0 = b * bs
            pd = psf.get()
            for c in range(8):
                mm(pd, pd[0:bs, 0:32], HT[:, c, c0:c0 + bs], wdv[:, c, :], c == 0, c == 7, [HT, wdt])
            s = s32_r.get()
            tt("dve", s, s[0:bs, :], pd[0:bs, 0:32], DTB[0:bs, :], ALU.add, [pd, DTB])
            act(s, s[0:bs, :], s[0:bs, :], AF.Exp, [s])
            act(DTs, DTs[0:bs, b, :], s[0:bs, :], AF.Ln, [s], bias=1.0, scale=1.0)
            tt("dve", DTA, DTA[0:bs, b, :], DTs[0:bs, b, :], ANEG[0:bs, :], ALU.mult, [DTs, ANEG])
        P.label = "odd.A"
        pendx = None
        for fc in range(24):
            w = wload("w_ox", j, fc, 8 * 128)
            wv = w[:, 0:1024].rearrange("p (c f) -> p c f", c=8)
            pt = psf.get()
            for c in range(8):
                mm(pt, pt[:, 0:n], wv[:, c, :], HT[:, c, 0:n], c == 0, c == 7, [w, HT])
            raw = rawx.get()
            cp("pool", raw, raw[:, 0:3], HALO_X[:, fc, :], [HALO_X])
            cp("act", raw, raw[:, 3:3 + n], pt[:, 0:n], [pt])
            cp("pool", HALO_X, HALO_X[:, fc, :], raw[:, n:n + 3], [raw])
            cv = cvx.get()
            ts("pool", cv, cv[:, 0:n], raw[:, 0:n], CWX[:, 4 * fc:4 * fc + 1], CBX[:, fc:fc + 1], ALU.mult, ALU.add, [raw, CWX, CBX])
            for k in (1, 2, 3):
                stt("dve", cv, cv[:, 0:n], raw[:, k:k + n], CWX[:, 4 * fc + k:4 * fc + k + 1], cv[:, 0:n], ALU.mult, ALU.add, [raw, CWX, cv])

            def tailx(cv=cv, fc=fc):
                act([XTK[(xgrp(fc), bb)] for bb in range(nblk)], XBC[:, fc, 0:n], cv[:, 0:n], AF.Silu, [cv])

            if pendx is not None:
                pendx()
            pendx = tailx
        pendx()
        P.label = "odd.dt"
        for b in range(nblk):
            pA = psf.get()
            mm(pA, pA[0:bs, 0:32], TRI[0:bs, 0:bs], DTA[0:bs, b, :], True, True, [TRI, DTA])
            pB = psf.get()
            mm(pB, pB[:, 0:32], ONF[0:bs, :], DTA[0:bs, b, :], True, True, [ONF, DTA])
            act(EXPC, EXPC[0:bs, b, :], pA[0:bs, 0:32], AF.Exp, [pA])
            act(CDEC, CDEC[:, b, :], pB[:, 0:32], AF.Exp, [pB])
            s2 = s32_r.get()
            cp("act", s2, s2[0:bs, :], pA[0:bs, 0:32], [pA])
            tt("dve", s2, s2[0:bs, :], pB[0:bs, 0:32], s2[0:bs, :], ALU.subtract, [pB, s2])
            act(s2, s2[0:bs, :], s2[0:bs, :], AF.Exp, [s2])
            tt("dve", WEND, WEND[0:bs, b, :], s2[0:bs, :], DTs[0:bs, b, :], ALU.mult, [s2, DTs])
        P.label = "odd.ssd"
        iters = [(g, b) for g in range(4) for b in range(nblk)]
        ctx = {}
        ctx2 = {}
        gctx = {}

        def front(g, b):
            if b == 0:
                wz = wload("w_oz", j, g, 8 * 512)
                NW = nw_r.get()
                P.dma("sp", NW[:, :], D["nw"][j][:, g * 512:(g + 1) * 512], writes=[NW])
                gctx[g] = (wz, NW)
            wz, NW = gctx[g]
            wzv = wz[:, 0:4096].rearrange("p (c f) -> p c f", c=8)
            c0 = b * bs
            px = psb.get()
            for i4 in range(4):
                trp(px, px[0:bs, i4 * 128:(i4 + 1) * 128], XBC[:, 4 * g + i4, c0:c0 + bs], IDB[:, :], [XTK[(g, b)], IDB])
            trp(px, px[0:bs, 512:640], XBC[:, 16 + g, c0:c0 + bs], IDB[:, :], [XTK[(4 + g, b)], IDB])
            xtok = xtok_r.get()
            cp("act", xtok, xtok[0:bs, :], px[0:bs, 0:512], [px])
            btok = btok_r.get()
            cp("act", btok, btok[0:bs, :], px[0:bs, 512:640], [px])
            yb = yb_r.get()
            tt("pool", yb, yb[0:bs, :, 0:bs], TRI[0:bs, None, 0:bs].to_broadcast([bs, 8, bs]),
               DTA[0:bs, b, 8 * g:8 * g + 8].unsqueeze(2).to_broadcast([bs, 8, bs]), ALU.mult, [TRI, DTA])
            xdt = xdt_r.get()
            tt("pool", xdt, xdt[0:bs, :].rearrange("p (h d) -> p h d", h=8), xtok[0:bs, :].rearrange("p (h d) -> p h d", h=8),
               DTs[0:bs, b, 8 * g:8 * g + 8].unsqueeze(2).to_broadcast([bs, 8, 64]), ALU.mult, [xtok, DTs])
            xw = xw_r.get()
            tt("pool", xw, xw[0:bs, :].rearrange("p (h d) -> p h d", h=8), xtok[0:bs, :].rearrange("p (h d) -> p h d", h=8),
               WEND[0:bs, b, 8 * g:8 * g + 8].unsqueeze(2).to_broadcast([bs, 8, 64]), ALU.mult, [xtok, WEND])
            xdsk = xdsk_r.get()
            tt("pool", xdsk, xdsk[0:bs, :].rearrange("p (h d) -> p h d", h=8), xtok[0:bs, :].rearrange("p (h d) -> p h d", h=8),
               DSK[0:bs, 8 * g:8 * g + 8].unsqueeze(2).to_broadcast([bs, 8, 64]), ALU.mult, [xtok, DSK])
            pcb = psf6.get()
            mm(pcb, pcb[0:bs, 0:bs], XBC[:, 16 + g, c0:c0 + bs], XBC[:, 20 + g, c0:c0 + bs], True, True, [XTK[(4 + g, b)], XTK[(8 + g, b)]])
            cbm = cbm_r.get()
            tt("dve", cbm, cbm[0:bs, 0:bs], pcb[0:bs, 0:bs], TRI[0:bs, 0:bs], ALU.mult, [pcb, TRI])
            lt = lt_r.get()
            for half in range(2):
                pseg = psf6.get()
                for r in range(4):
                    rr = half * 4 + r
                    mm(pseg, pseg[0:bs, r * bs:(r + 1) * bs], UB[0:bs, 0:bs], yb[0:bs, rr, 0:bs], True, True, [UB, yb])
                act(lt, lt[0:bs, half * 4:half * 4 + 4, 0:bs], pseg[0:bs, 0:4 * bs].rearrange("p (r q) -> p r q", r=4), AF.Exp, [pseg])
            mix = mix_r.get()
            tt("dve", mix, mix[0:bs, :, 0:bs], lt[0:bs, :, 0:bs], cbm[0:bs, None, 0:bs].to_broadcast([bs, 8, bs]), ALU.mult, [lt, cbm])
            pz = psf6.get()
            for c in range(8):
                mm(pz, pz[0:bs, :], HT[:, c, c0:c0 + bs], wzv[:, c, :], c == 0, c == 7, [HT, wz])
            sz = sz_r.get()
            sg = sg_r.get()
            act(sg, sg[0:bs, 0:512], pz[0:bs, :], AF.Exp, [pz], scale=-1.0)
            cp("act", sz, sz[0:bs, 0:512], pz[0:bs, :], [pz])
            act(sg, sg[0:bs, 0:512], sg[0:bs, 0:512], AF.Ln, [sg], bias=1.0, scale=1.0)
            act(sg, sg[0:bs, 0:512], sg[0:bs, 0:512], AF.Exp, [sg], scale=-1.0)
            tt("pool", sz, sz[0:bs, 0:512], sz[0:bs, 0:512], sg[0:bs, 0:512], ALU.mult, [sz, sg])
            ctx[(g, b)] = (xdsk, btok, xdt, xw, mix, sz, NW)

        def back(g, b):
            xdsk, btok, xdt, xw, mix, sz, NW = ctx.pop((g, b))
            c0 = b * bs
            py = psf6.get()
            mm(py, py[0:bs, 0:512], IDB[0:bs, 0:bs], xdsk[0:bs, 0:512], True, False, [IDB, xdsk])
            for r in range(8):
                mm(py, py[0:bs, r * 64:(r + 1) * 64], mix[0:bs, r, 0:bs], xdt[0:bs, r * 64:(r + 1) * 64], False, r == 7, [mix, xdt])
            pi_ = psf6.get()
            mm(pi_, pi_[0:bs, :], XBC[:, 20 + g, c0:c0 + bs], HTSB[:, g, :], True, True, [XTK[(8 + g, b)], HTSB])
            y1 = y1_r.get()
            tt("dve", y1, y1[0:bs, 0:512].rearrange("p (h d) -> p h d", h=8), pi_[0:bs, :].rearrange("p (h d) -> p h d", h=8),
               EXPC[0:bs, b, 8 * g:8 * g + 8].unsqueeze(2).to_broadcast([bs, 8, 64]), ALU.mult, [pi_, EXPC])
            tt("dve", y1, y1[0:bs, 0:512], y1[0:bs, 0:512], py[0:bs, :], ALU.add, [y1, py])
            pst = psf6.get()
            mm(pst, pst[:, :], btok[0:bs, :], xw[0:bs, :], True, True, [btok, xw])
            for r in range(8):
                stt("dve", HTS, HTS[:, g, r * 64:(r + 1) * 64], HTS[:, g, r * 64:(r + 1) * 64], CDEC[:, b, 8 * g + r:8 * g + r + 1],
                    pst[:, r * 64:(r + 1) * 64], ALU.mult, ALU.add, [HTS, CDEC, pst])
            cp("act", HTSB, HTSB[:, g, :], HTS[:, g, :], [HTS])
            tt("dve", y1, y1[0:bs, 0:512], y1[0:bs, 0:512], sz[0:bs, 0:512], ALU.mult, [y1, sz])
            sqs = sqs_r.get()
            act(sqs, sqs[0:bs, 0:512], y1[0:bs, 0:512], AF.Square, [y1])
            s1 = s1_r.get()
            red("dve", s1, s1[0:bs, :], sqs[0:bs, 0:512], [sqs])
            rstd_inplace("dve", s1, s1[0:bs, :], 1.0 / 512, [s1])
            ynb = ynb_r.get()
            stt("dve", ynb, ynb[0:bs, :], y1[0:bs, 0:512], s1[0:bs, 0:1], NW[0:bs, :], ALU.mult, ALU.mult, [y1, s1, NW])
            ctx2[(g, b)] = ynb

        def back2(g, b):
            ynb = ctx2.pop((g, b))
            c0 = b * bs
            pyt = psb.get()
            for i4 in range(4):
                trp(pyt, pyt[:, i4 * bs:(i4 + 1) * bs], ynb[0:bs, i4 * 128:(i4 + 1) * 128], IDB[0:bs, 0:bs], [ynb, IDB])
            cp("act", XTK[(g, b)], XBC[:, 4 * g:4 * g + 4, c0:c0 + bs], pyt[:, 0:4 * bs].rearrange("p (a t) -> p a t", a=4), [pyt])

        front(*iters[0])
        for k in range(len(iters)):
            if k + 1 < len(iters):
                front(*iters[k + 1])
            back(*iters[k])
            if k >= 1:
                back2(*iters[k - 1])
        back2(*iters[-1])
        P.label = "odd.C"
        for dc in range(8):
            prefetch(dc)
            w = wload("w_oo", j, dc, 16 * 128)
            wv = w[:, 0:2048].rearrange("p (c f) -> p c f", c=16)
            pt = psf.get()
            for cc in range(16):
                mm(pt, pt[:, 0:n], wv[:, cc, :], XBC[:, cc, 0:n], cc == 0, cc == 15, [w] + [XTK[(cc // 4, bb)] for bb in range(nblk)])
            resid_add(seq, ti, n, dc, pt)

    def odd_finish(seq, j):
        P.dma("sp", seq["o_xc"][j], HALO_X[:].rearrange("p a b -> p (a b)"), reads=[HALO_X])
        P.dma("sp", seq["o_ssm"][j], HTS[:].rearrange("p a b -> p (a b)"), reads=[HTS])

    seqs = []
    for b in range(NP):
        seqs.append(dict(sample=False, L=L, xin=D["xp"][b], yout=O["p_y"][b],
                         o_k=[O["p_k"][jj, b] for jj in range(2)], o_v=[O["p_v"][jj, b] for jj in range(2)],
                         o_lf=[O["p_lf"][jj, b] for jj in range(2)], o_sc=[O["p_sc"][jj, b] for jj in range(2)],
                         o_xc=[O["p_xc"][jj, b] for jj in range(2)], o_ssm=[O["p_ssm"][jj, b] for jj in range(2)],
                         o_fc=[O["p_fc"][ii, b] for ii in range(4)]))
    if with_sample:
        seqs.append(dict(sample=True, L=16, xin=D["xs"][0], yout=O["s_y"][0],
                         o_k=[O["s_k"][jj, 0] for jj in range(2)], o_v=[O["s_v"][jj, 0] for jj in range(2)],
                         o_lf=[O["s_lf"][jj, 0] for jj in range(2)], o_sc=[O["s_sc"][jj, 0] for jj in range(2)],
                         o_xc=[O["s_xc"][jj, 0] for jj in range(2)], o_ssm=[O["s_ssm"][jj, 0] for jj in range(2)],
                         o_fc=[O["s_fc"][ii, 0] for ii in range(4)]))
    def emit_casts(kind, i):
        if kind == "even":
            names = ["w_es", "w_eqk", "w_evf", "w_eo"]
        elif kind == "odd":
            names = ["w_ox", "w_oz", "w_odt", "w_oo"]
        else:
            names = ["w_fu", "w_fd"]
        for name in names:
            for sl in range(WSHAPES[name][1]):
                if (name, i, sl) in WTOK:
                    continue
                tok = T("%s_%d_%d" % (name, i, sl))
                WTOK[(name, i, sl)] = tok
                P.dma("pool", WB[name][i, sl], D[name][i, sl], writes=[tok])

    for si in range(min(2, len(plan))):
        emit_casts(*plan[si])
    normcfg = {"even": lambda i: ("g_mix", 2 * i), "odd": lambda i: ("g_mix", 2 * i + 1), "ffn": lambda i: ("g_ffn", i)}
    tilefn = {"even": even_tile, "odd": odd_tile, "ffn": ffn_tile}
    finfn = {"even": even_finish, "odd": odd_finish, "ffn": ffn_finish}
    hcount = [0]
    for qi, seq in enumerate(seqs):
        Ls = seq["L"]
        tiles = [(ti, min(TT, Ls)) for ti in range(max(1, Ls // TT))]
        xv = seq["xin"].rearrange("p (c t) -> p c t", c=8)
        for ti, n in tiles:
            P.dma("sp", xtile(seq, ti)[:, :, 0:n], xv[:, :, ti * TT:ti * TT + n], writes=[xtile(seq, ti)])
        steps = [(pi, kind, i, ti, n) for pi, (kind, i) in enumerate(plan) for ti, n in tiles]
        ready = {}

        def do_norm(sidx, part=None):
            pi, kind, i, ti, n = steps[sidx]
            gname, layer = normcfg[kind](i)
            if part is None:
                hb = hcount[0] % 2
                hcount[0] += 1
                rmsnorm(seq, ti, n, gname, layer, HTS2[hb], GMS[hb])
                ready[sidx] = HTS2[hb]
            elif part < 4:
                norm_chunks(seq, ti, n, [2 * part, 2 * part + 1])
            else:
                hb = hcount[0] % 2
                hcount[0] += 1
                norm_final(seq, ti, n, gname, layer, HTS2[hb], GMS[hb])
                ready[sidx] = HTS2[hb]

        for sidx, (pi, kind, i, ti, n) in enumerate(steps):
            if qi == 0 and ti == 0 and pi + 2 < len(plan):
                emit_casts(*plan[pi + 2])
            if sidx not in ready:
                do_norm(sidx)
            HTc = ready.pop(sidx)

            def prefetch(dc, sidx=sidx, ti=ti):
                nx = sidx + 1
                if nx < len(steps) and steps[nx][3] != ti and dc <= 4:
                    do_norm(nx, dc)

            tilefn[kind](seq, ti, n, i, ti == 0, HTc, prefetch)
            if ti == tiles[-1][0]:
                finfn[kind](seq, i)
        yv = seq["yout"].rearrange("p (c t) -> p c t", c=8)
        for ti, n in tiles:
            P.dma("sp", yv[:, :, ti * TT:ti * TT + n], xtile(seq, ti)[:, :, 0:n], reads=[xtile(seq, ti)])
    info = P.emit()
    info["pe_labels"] = [P.labels[i] for i in range(len(P.ops)) if P.ops[i][0] == "pe"]
    return nc, info


def _consts():
    idx = np.arange(128)
    c = {}
    c["c_ident"] = np.eye(128, dtype=np.float32)
    c["c_tri"] = (idx[:, None] <= idx[None, :]).astype(np.float32)
    c["c_ones"] = np.ones((128, 128), np.float32)
    c["c_maskneg"] = np.where(idx[None, :] >= idx[:, None], 0.0, -240000.0).astype(np.float32)
    c["c_u"] = (idx[:, None] > idx[None, :]).astype(np.float32)
    sel = np.zeros((8, 8, 128), np.float32)
    for h in range(8):
        sel[h, h, :] = 1.0
    c["c_sel"] = sel.reshape(8, 8 * 128)
    return c


def _fm(v):
    s = v.shape
    nch = s[-1] // 128
    return np.ascontiguousarray(np.moveaxis(v.reshape(s[:-1] + (nch, 128)), -1, -2))


def _wsplit(w, cols_list):
    K = w.shape[0]
    kc = K // 128
    out = []
    for cols in cols_list:
        sub = w[:, cols].reshape(kc, 128, -1)
        out.append(np.ascontiguousarray(np.transpose(sub, (1, 0, 2))).reshape(128, -1))
    return np.stack(out)


def prep_shared(inp):
    f = lambda a: np.asarray(a, dtype=np.float32)
    S = dict(_consts())
    wie, woe, wio, woo, wu, wd = f(inp["w_in_even"]), f(inp["w_out_even"]), f(inp["w_in_odd"]), f(inp["w_out_odd"]), f(inp["w_up"]), f(inp["w_down"])
    ar = np.arange
    S["w_es"] = np.stack([_wsplit(wie[j], [np.concatenate([k * 512 + cc * 128 + ar(128) for k in range(3)]) for cc in range(4)]) for j in range(2)])
    S["w_eqk"] = np.stack([_wsplit(wie[j], [1536 + ar(512), 2048 + ar(512)]) for j in range(2)])
    S["w_evf"] = np.stack([_wsplit(wie[j], [2560 + ar(520)]) for j in range(2)])
    S["w_eo"] = np.stack([_wsplit(woe[j], [dc * 128 + ar(128) for dc in range(8)]) for j in range(2)])
    S["w_ox"] = np.stack([_wsplit(wio[j], [2048 + fc * 128 + ar(128) for fc in range(24)]) for j in range(2)])
    S["w_oz"] = np.stack([_wsplit(wio[j], [g * 512 + ar(512) for g in range(4)]) for j in range(2)])
    S["w_odt"] = np.stack([_wsplit(wio[j], [5120 + ar(32)]) for j in range(2)])
    S["w_oo"] = np.stack([_wsplit(woo[j], [dc * 128 + ar(128) for dc in range(8)]) for j in range(2)])
    S["w_fu"] = np.stack([_wsplit(wu[i], [np.concatenate([k * 2816 + jj * 128 + ar(128) for k in range(2)]) for jj in range(22)]) for i in range(4)])
    S["w_fd"] = np.stack([_wsplit(wd[i], [dc * 128 + ar(128) for dc in range(8)]) for i in range(4)])
    rep = lambda v: np.ascontiguousarray(np.broadcast_to(v[:, None, :], (v.shape[0], 128, v.shape[1])))
    S["g_mix"] = _fm(f(inp["norm_mix"]))
    S["g_ffn"] = _fm(f(inp["norm_ffn"]))
    S["cw_a"] = np.ascontiguousarray(np.moveaxis(_fm(f(inp["conv_a_w"])), 1, 3)).reshape(2, 128, 12)
    S["qg"] = rep(f(inp["q_norm"]))
    S["kg"] = rep(f(inp["k_norm"]))
    S["bfg"] = rep(f(inp["b_forget"]))
    S["cw_x"] = np.ascontiguousarray(np.moveaxis(_fm(f(inp["ssm_conv_w"])), 1, 3)).reshape(2, 128, 96)
    S["cb_x"] = _fm(f(inp["ssm_conv_b"]))
    S["dtb"] = rep(f(inp["dt_bias"]))
    S["alog"] = rep(f(inp["a_log"]))
    S["dsk"] = rep(f(inp["d_skip"]))
    S["nw"] = rep(f(inp["ssm_norm"]))
    S["cw_f"] = np.ascontiguousarray(np.moveaxis(_fm(f(inp["ffn_conv_w"])), 1, 3)).reshape(4, 128, 66)
    S["cb_f"] = _fm(f(inp["ffn_conv_b"]))
    return S


def prep_core(inp, S, pb, sb_, L):
    f = lambda a: np.asarray(a, dtype=np.float32)
    m = dict(S)
    xp = f(inp["x_prompt"])[pb][:, :L]
    m["xp"] = np.ascontiguousarray(np.transpose(xp.reshape(len(pb), L, 8, 128), (0, 3, 2, 1))).reshape(len(pb), 128, 8 * L)
    xs = f(inp["x_sample"])[sb_:sb_ + 1]
    m["xs"] = np.ascontiguousarray(np.transpose(xs.reshape(1, 16, 8, 128), (0, 3, 2, 1))).reshape(1, 128, 128)
    ck = f(inp["cache_fox_k"])[:, sb_]
    ckt = ck.reshape(2, PAST, 4, 2, 64)
    m["ckT"] = np.ascontiguousarray(np.transpose(ckt, (0, 3, 4, 2, 1))).reshape(2, 128, 4 * PAST)
    m["cv"] = np.ascontiguousarray(f(inp["cache_fox_v"])[:, sb_].reshape(2, PAST, 512))
    m["clf"] = np.ascontiguousarray(f(inp["cache_fox_logf"])[:, sb_])
    m["st_sc"] = np.ascontiguousarray(np.moveaxis(_fm(f(inp["state_sconv"])[:, sb_]), 1, 3)).reshape(2, 128, 8)
    m["st_xc"] = np.ascontiguousarray(np.moveaxis(_fm(f(inp["state_ssm_conv"])[:, sb_]), 1, 3)).reshape(2, 128, 72)
    m["st_ssm"] = np.ascontiguousarray(np.transpose(f(inp["state_ssm"])[:, sb_].reshape(2, 2048, 128), (0, 2, 1)))
    m["st_fc"] = np.ascontiguousarray(np.moveaxis(_fm(f(inp["state_ffn_conv"])[:, sb_]), 1, 3)).reshape(4, 128, 44)
    return m


def _unfm(a, w):
    s = a.shape
    nch = s[-1] // w
    a = a.reshape(s[:-2] + (128, nch, w))
    a = np.moveaxis(a, -1, -3)
    a = np.swapaxes(a, -1, -2)
    return np.ascontiguousarray(a).reshape(s[:-2] + (w, nch * 128))


def unpack(results, L, NP):
    def cat(key, axis):
        return np.concatenate([r[key] for r in results], axis=axis)
    out = {}
    for pre, ll in (("p", L), ("s", 16)):
        y = cat(pre + "_y", 0)
        B = y.shape[0]
        out[pre + "_y"] = np.ascontiguousarray(np.transpose(y.reshape(B, 128, 8, ll), (0, 3, 2, 1))).reshape(B, ll, 1024)
        out[pre + "_k"] = cat(pre + "_k", 1).reshape(2, B, ll, 8, 64)
        out[pre + "_v"] = cat(pre + "_v", 1).reshape(2, B, ll, 8, 64)
        out[pre + "_lf"] = cat(pre + "_lf", 1)
        out[pre + "_sc"] = _unfm(cat(pre + "_sc", 1), 2)
        out[pre + "_xc"] = _unfm(cat(pre + "_xc", 1), 3)
        ssm = cat(pre + "_ssm", 1)
        out[pre + "_ssm"] = np.ascontiguousarray(np.swapaxes(ssm, -1, -2)).reshape(2, B, 32, 64, 128)
        out[pre + "_fc"] = _unfm(cat(pre + "_fc", 1), 2)
    return out


def kernel(**inputs):
    NCORES = 8
    L = 2048
    NP = 4
    S = prep_shared(inputs)
    in_maps = [prep_core(inputs, S, list(range(c * NP, (c + 1) * NP)), c, L) for c in range(NCORES)]
    nc, info = build(NP, L)
    res = run_bass_kernel_spmd(nc, in_maps, core_ids=list(range(NCORES)))
    o = unpack(res.results, L, NP)
    return (o["p_y"], o["s_y"], o["p_k"], o["p_v"], o["p_lf"], o["p_sc"], o["p_xc"], o["p_ssm"], o["p_fc"],
            o["s_k"], o["s_v"], o["s_lf"], o["s_sc"], o["s_xc"], o["s_ssm"], o["s_fc"])
```
